# Optimizing a Trainium2 kernel written in Bass

```python
import jax, jax.numpy as jnp
from jax import lax
import numpy as np

D_MODEL = 4096
BATCH = 1
SEQ = 8192
DEPTH = 2

MIX_WIDTH = D_MODEL
RET_WIDTH = MIX_WIDTH // 2
GLA_WIDTH = MIX_WIDTH - RET_WIDTH
RET_HEADS = 8
RET_DK = RET_WIDTH // RET_HEADS
RET_DV = RET_WIDTH // RET_HEADS
GLA_HEADS = 4
GLA_DV = GLA_WIDTH // GLA_HEADS
GLA_DK = GLA_DV // 2
GLA_KEY_WIDTH = GLA_HEADS * GLA_DK
GLA_GATE_RANK = 16
GLA_GATE_TAU = 16.0
_FFN_RAW = -(-8 * D_MODEL // 3)
FFN_HIDDEN = -(-_FFN_RAW // 256) * 256
RET_CHUNK = 128
GLA_CHUNK = 64
ROPE_BASE = 10000.0
EPS = 1e-6

IN_SPLITS = (RET_WIDTH, RET_WIDTH, RET_WIDTH, RET_WIDTH,
             GLA_KEY_WIDTH, GLA_KEY_WIDTH, GLA_WIDTH, GLA_WIDTH,
             GLA_GATE_RANK)
IN_WIDTH = sum(IN_SPLITS)

kernel_name = "hybrid_retention_gla_parallel_heads"


def rms_norm(x, gain):
    xf = x.astype(jnp.float32)
    y = xf * lax.rsqrt(jnp.mean(xf * xf, axis=-1, keepdims=True) + EPS)
    return (y * gain.astype(jnp.float32)).astype(x.dtype)


def apply_rotary(t, positions):
    half = t.shape[-1] // 2
    inv_freq = ROPE_BASE ** (-jnp.arange(half, dtype=jnp.float32) / half)
    ang = positions.astype(jnp.float32)[..., None] * inv_freq
    cos = jnp.cos(ang)[:, :, None, :]
    sin = jnp.sin(ang)[:, :, None, :]
    t1, t2 = t[..., :half], t[..., half:]
    return jnp.concatenate([t1 * cos - t2 * sin, t1 * sin + t2 * cos], axis=-1)


def to_chunks(t, c):
    b, s, h, d = t.shape
    return t.reshape(b, s // c, c, h, d).transpose(1, 0, 3, 2, 4)


def from_chunks(t):
    n, b, h, c, d = t.shape
    return t.transpose(1, 0, 3, 2, 4).reshape(b, n * c, h, d)


def retention_chunkwise(q, k, v):
    b, s, h, dk = q.shape
    dv = v.shape[-1]
    c = RET_CHUNK
    log_gamma = jnp.log1p(-jnp.exp2(-5.0 - jnp.arange(h, dtype=jnp.float32)))
    k = k * (dk ** -0.5)
    idx = jnp.arange(c, dtype=jnp.float32)
    diff = idx[:, None] - idx[None, :]
    decay = jnp.where(diff >= 0,
                      jnp.exp(log_gamma[:, None, None] * jnp.maximum(diff, 0.0)),
                      0.0)
    q_decay = jnp.exp(log_gamma[:, None] * (idx + 1.0))[None, :, :, None]
    k_decay = jnp.exp(log_gamma[:, None] * (c - 1.0 - idx))[None, :, :, None]
    chunk_decay = jnp.exp(log_gamma * c)[None, :, None, None]

    def step(state, inp):
        qb, kb, vb = inp
        scores = jnp.einsum('bhid,bhjd->bhij', qb, kb) * decay
        out = (jnp.einsum('bhij,bhjv->bhiv', scores, vb)
               + jnp.einsum('bhid,bhdv->bhiv', qb, state) * q_decay)
        state = state * chunk_decay + jnp.einsum('bhjd,bhjv->bhdv', kb * k_decay, vb)
        return state, out

    s0 = jnp.zeros((b, h, dk, dv), jnp.float32)
    _, o = lax.scan(step, s0, (to_chunks(q, c), to_chunks(k, c), to_chunks(v, c)))
    return from_chunks(o)


def gla_chunked(q, k, v, log_a):
    b, s, h, dk = q.shape
    dv = v.shape[-1]
    c = GLA_CHUNK
    q = q * (dk ** -0.5)
    causal = jnp.tril(jnp.ones((c, c), dtype=bool))[:, :, None]

    def step(state, inp):
        qb, kb, vb, ab = inp
        cum = jnp.cumsum(ab, axis=2)
        rel = cum[:, :, :, None, :] - cum[:, :, None, :, :]
        w = jnp.exp(jnp.where(causal, rel, -jnp.inf))
        attn = jnp.einsum('bhid,bhmd,bhimd->bhim', qb, kb, w)
        out = (jnp.einsum('bhim,bhmv->bhiv', attn, vb)
               + jnp.einsum('bhid,bhdv->bhiv', qb * jnp.exp(cum), state))
        last = cum[:, :, -1:, :]
        state = (state * jnp.exp(last)[:, :, 0, :, None]
                 + jnp.einsum('bhmd,bhmv->bhdv', kb * jnp.exp(last - cum), vb))
        return state, out

    s0 = jnp.zeros((b, h, dk, dv), jnp.float32)
    _, o = lax.scan(step, s0, (to_chunks(q, c), to_chunks(k, c),
                               to_chunks(v, c), to_chunks(log_a, c)))
    return from_chunks(o)


def head_group_norm(o, gain):
    mu = jnp.mean(o, axis=-1, keepdims=True)
    var = jnp.mean(jnp.square(o - mu), axis=-1, keepdims=True)
    y = (o - mu) * lax.rsqrt(var + EPS)
    return y.reshape(o.shape[0], o.shape[1], -1) * gain


def head_rms_norm(o, gain):
    y = o * lax.rsqrt(jnp.mean(o * o, axis=-1, keepdims=True) + EPS)
    return y.reshape(o.shape[0], o.shape[1], -1) * gain


def hybrid_mixer(h, positions, w_in, gla_w_up, gla_b, ret_gain, gla_gain, w_out):
    b, s, _ = h.shape
    proj = (h @ w_in).astype(jnp.float32)
    cuts = list(np.cumsum(IN_SPLITS)[:-1])
    rq, rk, rv, rg, gq, gk, gv, gg, ga = jnp.split(proj, cuts, axis=-1)

    rq = apply_rotary(rq.reshape(b, s, RET_HEADS, RET_DK), positions)
    rk = apply_rotary(rk.reshape(b, s, RET_HEADS, RET_DK), positions)
    rv = rv.reshape(b, s, RET_HEADS, RET_DV)
    r_out = retention_chunkwise(rq, rk, rv)
    r_out = jax.nn.silu(rg) * head_group_norm(r_out, ret_gain.astype(jnp.float32))

    gate_logits = ga @ gla_w_up.astype(jnp.float32) + gla_b.astype(jnp.float32)
    log_a = (jax.nn.log_sigmoid(gate_logits) / GLA_GATE_TAU).reshape(b, s, GLA_HEADS, GLA_DK)
    g_out = gla_chunked(gq.reshape(b, s, GLA_HEADS, GLA_DK),
                        gk.reshape(b, s, GLA_HEADS, GLA_DK),
                        gv.reshape(b, s, GLA_HEADS, GLA_DV), log_a)
    g_out = jax.nn.silu(gg) * head_rms_norm(g_out, gla_gain.astype(jnp.float32))

    mixed = jnp.concatenate([r_out, g_out], axis=-1).astype(h.dtype)
    return mixed @ w_out


def swiglu(h, w_gate, w_up, w_down):
    return (jax.nn.silu(h @ w_gate) * (h @ w_up)) @ w_down


def setup_inputs(seed: int = 0) -> dict:
    key = jax.random.key(seed)
    ks = jax.random.split(key, 14)
    f32 = jnp.float32
    x = jax.random.normal(ks[0], (BATCH, SEQ, D_MODEL), f32)
    positions = jnp.broadcast_to(jnp.arange(SEQ, dtype=jnp.int32), (BATCH, SEQ))
    mix_norm = 1.0 + 0.02 * jax.random.normal(ks[1], (DEPTH, D_MODEL), f32)
    w_in = jax.random.normal(ks[2], (DEPTH, D_MODEL, IN_WIDTH), f32) * D_MODEL ** -0.5
    gla_w_up = jax.random.normal(ks[3], (DEPTH, GLA_GATE_RANK, GLA_KEY_WIDTH), f32) * GLA_GATE_RANK ** -0.5
    gla_b = 0.1 * jax.random.normal(ks[4], (DEPTH, GLA_KEY_WIDTH), f32)
    ret_gain = 1.0 + 0.02 * jax.random.normal(ks[5], (DEPTH, RET_WIDTH), f32)
    gla_gain = 1.0 + 0.02 * jax.random.normal(ks[6], (DEPTH, GLA_WIDTH), f32)
    w_out = jax.random.normal(ks[7], (DEPTH, MIX_WIDTH, D_MODEL), f32) * MIX_WIDTH ** -0.5
    ffn_norm = 1.0 + 0.02 * jax.random.normal(ks[8], (DEPTH, D_MODEL), f32)
    w_gate = jax.random.normal(ks[9], (DEPTH, D_MODEL, FFN_HIDDEN), f32) * D_MODEL ** -0.5
    w_up = jax.random.normal(ks[10], (DEPTH, D_MODEL, FFN_HIDDEN), f32) * D_MODEL ** -0.5
    w_down = jax.random.normal(ks[11], (DEPTH, FFN_HIDDEN, D_MODEL), f32) * FFN_HIDDEN ** -0.5
    final_norm = 1.0 + 0.02 * jax.random.normal(ks[12], (D_MODEL,), f32)
    return {"x": x, "positions": positions, "mix_norm": mix_norm, "w_in": w_in,
            "gla_w_up": gla_w_up, "gla_b": gla_b, "ret_gain": ret_gain,
            "gla_gain": gla_gain, "w_out": w_out, "ffn_norm": ffn_norm,
            "w_gate": w_gate, "w_up": w_up, "w_down": w_down,
            "final_norm": final_norm}


def reference(x, positions, mix_norm, w_in, gla_w_up, gla_b, ret_gain, gla_gain,
              w_out, ffn_norm, w_gate, w_up, w_down, final_norm):
    for l in range(DEPTH):
        h = rms_norm(x, mix_norm[l])
        x = x + hybrid_mixer(h, positions, w_in[l], gla_w_up[l], gla_b[l],
                             ret_gain[l], gla_gain[l], w_out[l])
        h = rms_norm(x, ffn_norm[l])
        x = x + swiglu(h, w_gate[l], w_up[l], w_down[l])
    return rms_norm(x, final_norm)
```

```python
import math
from contextlib import ExitStack

import numpy as np
import concourse.bass as bass
import concourse.mybir as mybir
from concourse.bass_utils import run_bass_kernel_spmd

F32 = mybir.dt.float32
BF16 = mybir.dt.bfloat16
I32 = mybir.dt.int32
ALU = mybir.AluOpType
AF = mybir.ActivationFunctionType

NCORES = 8
DEPTH = 2
D = 4096
SEQ = 8192
T = SEQ // NCORES
NT = T // 128
KC = D // 128
INW = 14352
FF = 11008
FCH = FF // 128
RH, GH = 8, 4
O_RQ, O_RK, O_RV, O_RG = 0, 2048, 4096, 6144
O_GQ, O_GK, O_GV, O_GG, O_GA = 8192, 9216, 10240, 12288, 14336
EPS = 1e-6
AGF = 4096 + 4096 + 8
GAM = [1.0 - 2.0 ** (-5.0 - h) for h in range(RH)]
TWO_PI = 2.0 * math.pi


class Buf:
    __slots__ = ("name", "w", "rs")

    def __init__(self, name=""):
        self.name = name
        self.w = None
        self.rs = {}


class _Eng:
    def __init__(self, name):
        self.name = name
        self.sems = []
        self.cnt = 0
        self.pending = False
        self.seen = {}
        self.prog = []


class Sched:
    EPOCH = 30000

    def __init__(self):
        self.E = {n: _Eng(n) for n in ("pe", "dve", "act", "pool", "sp")}
        self.nsem = 0
        self.owner = {}
        self.rings = []
        for e in self.E.values():
            self._new_epoch(e)

    def _new_sem(self):
        k = self.nsem
        self.nsem += 1
        return k

    def _new_epoch(self, e):
        k = self._new_sem()
        e.sems.append(k)
        e.cnt = 0
        self.owner[k] = e.name

    def _collect(self, E, R, W):
        waits = {}

        def need(ev, raw):
            if ev is None:
                return
            sem, val = ev
            if self.owner.get(sem) == E.name and E.name == "pe":
                return
            if E.seen.get(sem, 0) >= val:
                return
            if waits.get(sem, 0) < val:
                waits[sem] = val

        for b in R:
            need(b.w, True)
        for b in W:
            need(b.w, False)
            for sem, val in b.rs.items():
                need((sem, val), False)
        for sem, val in waits.items():
            E.seen[sem] = val
        return waits

    @staticmethod
    def _mark(ev, R, W):
        sem, val = ev
        for b in R:
            if b.rs.get(sem, 0) < val:
                b.rs[sem] = val
        for b in W:
            b.w = ev
            b.rs = {}

    def op(self, e, fn, R=(), W=(), inc=True):
        E = self.E[e]
        if inc and not E.pending and E.cnt >= self.EPOCH:
            self._new_epoch(E)
        waits = self._collect(E, R, W)
        sem = E.sems[-1]
        if inc:
            E.cnt += 1
            E.pending = False
            ev = (sem, E.cnt)
            E.prog.append((list(waits.items()), fn, (sem, 1)))
        else:
            E.pending = True
            ev = (sem, E.cnt + 1)
            E.prog.append((list(waits.items()), fn, None))
        self._mark(ev, R, W)
        return ev

    def new_ring(self, n):
        r = {"sems": [self._new_sem() for _ in range(n)], "vals": [0] * n, "i": 0}
        self.rings.append(r)
        return r

    def dma(self, q, fn, R, W, ring, inc=16):
        E = self.E[q]
        j = ring["i"] % len(ring["sems"])
        ring["i"] += 1
        sem = ring["sems"][j]
        waits = self._collect(E, R, W)
        prev = ring["vals"][j]
        if prev > 0 and E.seen.get(sem, 0) < prev:
            waits[sem] = max(waits.get(sem, 0), prev)
            E.seen[sem] = prev
        ring["vals"][j] = prev + inc
        ev = (sem, prev + inc)
        E.prog.append((list(waits.items()), fn, (sem, inc)))
        self._mark(ev, R, W)
        return ev

    def barrier(self, engines=("pe", "dve", "act")):
        evs = []
        for n in engines:
            E = self.E[n]
            assert not E.pending
            if E.cnt > 0:
                evs.append((E.sems[-1], E.cnt))
        for r in self.rings:
            for s, v in zip(r["sems"], r["vals"]):
                if v > 0:
                    evs.append((s, v))
        for n in engines:
            E = self.E[n]
            waits = [(s, v) for (s, v) in evs
                     if self.owner.get(s) != n and E.seen.get(s, 0) < v]
            for s, v in waits:
                E.seen[s] = v
            if waits:
                E.prog.append((waits, None, None))

    def emit(self, nc):
        assert not any(E.pending for E in self.E.values())
        with ExitStack() as es:
            sems = [es.enter_context(nc.semaphore(f"s{k}")) for k in range(self.nsem)]
            block = es.enter_context(nc.Block())

            def run(E):
                def body(eng):
                    for waits, fn, inc in E.prog:
                        for s, v in waits:
                            eng.wait_ge(sems[s], v)
                        if fn is not None:
                            ins = fn(eng)
                            if inc is not None:
                                ins.then_inc(sems[inc[0]], inc[1])
                return body

            block.tensor(run(self.E["pe"]))
            block.vector(run(self.E["dve"]))
            block.scalar(run(self.E["act"]))
            block.gpsimd(run(self.E["pool"]))
            block.sync(run(self.E["sp"]))


class _Stop(Exception):
    pass


def build_program(layers=(0, 1), last=True, skip_ffn=False, stop_at=None):
    nc = bass.Bass("TRN2", target_bir_lowering=False)
    S = Sched()

    def din(name, shape, dt=F32):
        return nc.dram_tensor(name, shape, dt, kind="ExternalInput")

    x_in = din("x", [T, D])
    pos_in = din("pos", [1, T], I32)
    WS = [("w_in", D, INW), ("w_out", D, D), ("w_gate", D, FF), ("w_up", D, FF), ("w_down", FF, D)]
    w_sh = {n: din(n, [DEPTH, R_ // NCORES, C_]) for (n, R_, C_) in WS}
    wsh32 = {(l, n): nc.dram_tensor(f"{n}_sh{l}", [R_ // NCORES, C_ // 2], F32) for l in layers for (n, R_, C_) in WS}
    wfull32 = {(l, n): nc.dram_tensor(f"{n}_full{l}", [R_, C_ // 2], F32) for l in layers for (n, R_, C_) in WS}
    B_wfull = {k: Buf() for k in wfull32}
    mixn_d = din("mixn", [DEPTH, 128, KC])
    ffnn_d = din("ffnn", [DEPTH, 128, KC])
    finn_d = din("finn", [1, D])
    rgain_d = din("rgain", [DEPTH, 1, 2048])
    ggain_d = din("ggain", [DEPTH, 1, 2048])
    glab_d = din("glab", [DEPTH, 128, 8])
    wup_d = din("wup", [DEPTH, 16, 1024])
    invf_d = din("invf", [128, 1])
    ident_d = din("ident", [128, 128])
    cmask_d = din("cmask", [128, 128])
    scanm_d = din("scanm", [128, T])
    gq_d = din("gq", [128, RH * 128])
    gk_d = din("gk", [128, RH * 128])
    corem_d = din("corem", [128, 8])
    out_d = nc.dram_tensor("out", [T, D], F32, kind="ExternalOutput")

    xs_d = nc.dram_tensor("xs", [T, D], F32)
    oloc_d = nc.dram_tensor("oloc", [T, D], F32)
    sg_d = nc.dram_tensor("sgd", [T, D], BF16)
    qt_d = nc.dram_tensor("qtd", [12, 128, 2, T], BF16)
    act_d = nc.dram_tensor("actd", [FCH, 128, T], BF16)
    agin_d = [nc.dram_tensor(f"agin{l}", [128, AGF], F32) for l in range(DEPTH)]
    agout_d = [nc.dram_tensor(f"agout{l}", [NCORES * 128, AGF], F32) for l in range(DEPTH)]

    B_xs = [Buf(f"xs{i}") for i in range(8)]
    B_oloc = [Buf() for _ in range(12)]
    B_sg = [Buf() for _ in range(12)]
    B_qt = [Buf() for _ in range(12)]
    B_act = [Buf() for _ in range(FCH // 2)]
    B_agin = [Buf() for _ in range(DEPTH)]
    B_agout = [Buf() for _ in range(DEPTH)]
    B_out = Buf("out")

    with ExitStack() as es:
        def sb(name, shape, dt=F32):
            return es.enter_context(nc.sbuf_tensor(name, shape, dt))

        actT = sb("actT", [128, KC, T], BF16)
        B_actT = Buf("actT")
        NW = 5
        KP = 4
        wslot32 = [sb(f"wslot{i}", [128, KP * 256], F32) for i in range(NW)]
        wslot = [w_[:, :].bitcast(BF16).rearrange("p (k n) -> p k n", n=512) for w_ in wslot32]
        wslotf = [w_[:, :].rearrange("p (k n) -> p k n", n=256) for w_ in wslot32]
        B_w = [(Buf(), Buf()) for _ in range(NW)]
        cosT = sb("cosT", [128, T])
        sinT = sb("sinT", [128, T])
        scanm = sb("scanm_t", [128, T])
        gq_t = sb("gq_t", [128, RH, 128])
        gk_t = sb("gk_t", [128, RH, 128])
        identb = sb("identb", [128, 128], BF16)
        cmask = sb("cmask_t", [128, 128])
        corem = sb("corem_t", [128, 8])
        invf = sb("invf_t", [128, 1])
        epsb = sb("epsb", [128, 1])
        mixn = sb("mixn_t", [128, DEPTH, KC])
        ffnn = sb("ffnn_t", [128, DEPTH, KC])
        negb = sb("negb", [128, DEPTH, 8])
        elall = sb("elall", [128, 8, 8])
        B_elall = Buf("elall")
        ajr = sb("ajr", [128, RH, 8])
        B_const = Buf("const")
        ARENA = 80 * 1024
        arena = sb("arena", [128, ARENA // 2], BF16)
        banks = [es.enter_context(nc.psum_tensor(f"bank{i}", [128, 512], F32)) for i in range(8)]
        B_bank = [Buf(f"bank{i}") for i in range(8)]

        state = {"off": 0}

        def arena_reset():
            S.barrier()
            state["off"] = 0

        def carve(shape, dt=F32):
            n = 1
            for s in shape[1:]:
                n *= s
            nb = n * (4 if dt in (F32, I32) else 2)
            nb = (nb + 63) // 64 * 64
            off = state["off"]
            assert off + nb <= ARENA, (off, nb)
            state["off"] = off + nb
            v = arena[:, off // 2:(off + nb) // 2]
            if dt != BF16:
                v = v.bitcast(dt)
            v = v[:, 0:n]
            if len(shape) == 3:
                v = v.rearrange("p (a b) -> p a b", b=shape[2])
            elif len(shape) == 4:
                v = v.rearrange("p (a b c) -> p a b c", b=shape[2], c=shape[3])
            return v

        ld_ring = S.new_ring(8)
        st_ring = S.new_ring(8)
        w_ring = S.new_ring(NW * 2)
        cc_ring = S.new_ring(4)
        cv_ring = S.new_ring(4)
        cvl_ring = S.new_ring(4)

        def dma(q, out, in_, R, W, ring):
            return S.dma(q, lambda e: e.dma_start(out=out, in_=in_), R, W, ring)

        def load(out, in_, R, W):
            return dma("act", out, in_, R, W, ld_ring)

        def store(out, in_, R, W):
            return dma("act", out, in_, R, W, st_ring)

        def tt(eng, out, in0, in1, op, R, W):
            S.op(eng, lambda e: e.tensor_tensor(out=out, in0=in0, in1=in1, op=op), R, W)

        def ts(eng, out, in0, s1, s2, op0, op1, R, W):
            if s2 is None:
                S.op(eng, lambda e: e.tensor_scalar(out=out, in0=in0, scalar1=s1, scalar2=None, op0=op0), R, W)
            else:
                S.op(eng, lambda e: e.tensor_scalar(out=out, in0=in0, scalar1=s1, scalar2=s2, op0=op0, op1=op1), R, W)

        def stt(eng, out, in0, scalar, in1, op0, op1, R, W):
            S.op(eng, lambda e: e.scalar_tensor_tensor(out=out, in0=in0, scalar=scalar, in1=in1, op0=op0, op1=op1), R, W)

        def act(out, in_, func, R, W, scale=None, bias=None, accum=None):
            kw = {}
            if scale is not None:
                kw["scale"] = scale
            if bias is not None:
                kw["bias"] = bias
            if accum is not None:
                kw["accum_out"] = accum
            S.op("act", lambda e: e.activation(out=out, in_=in_, func=func, **kw), R, W)

        def cpy(eng, out, in_, R, W):
            if eng == "act":
                act(out, in_, AF.Copy, R, W)
            else:
                S.op(eng, lambda e: e.tensor_copy(out=out, in_=in_), R, W)

        def mm(out, lhsT, rhs, start, stop, R, W, inc=True):
            S.op("pe", lambda e: e.matmul(out, lhsT=lhsT, rhs=rhs, start=start, stop=stop), R, W, inc=inc)

        def tr(out, in_, R, W, inc=True):
            S.op("pe", lambda e: e.transpose(out=out, in_=in_, identity=identb[:]), R + [B_const], W, inc=inc)

        def memset(eng, ap, val, W):
            S.op(eng, lambda e: e.memset(ap, val), [], W)

        def rstd_from(ss, n_el, out, B_ss, B_o):
            act(out, ss, AF.Ln, [B_ss, B_const], [B_o], scale=1.0 / n_el, bias=epsb[:, 0:1])
            act(out, out, AF.Exp, [B_o], [B_o], scale=-0.5)

        state["off"] = 0
        tmpf = carve([128, 128])
        B_t = Buf()
        load(tmpf, ident_d[:, :], [], [B_t])
        cpy("dve", identb[:], tmpf, [B_t], [B_const])
        load(cmask[:], cmask_d[:, :], [], [B_const])
        load(scanm[:], scanm_d[:, :], [], [B_const])
        load(gq_t[:], gq_d.ap().rearrange("p (h r) -> p h r", r=128), [], [B_const])
        load(gk_t[:], gk_d.ap().rearrange("p (h r) -> p h r", r=128), [], [B_const])
        load(corem[:], corem_d[:, :], [], [B_const])
        load(invf[:], invf_d[:, :], [], [B_const])
        load(mixn[:], mixn_d.ap().rearrange("l p c -> p l c"), [], [B_const])
        load(ffnn[:], ffnn_d.ap().rearrange("l p c -> p l c"), [], [B_const])
        load(negb[:], glab_d.ap().rearrange("l p c -> p l c"), [], [B_const])
        memset("dve", epsb[:], EPS, [B_const])
        ts("dve", negb[:], negb[:], -1.0, None, ALU.mult, None, [B_const], [B_const])
        for h in range(RH):
            ts("dve", ajr[:, h, :], corem[:], GAM[h] ** 1024 - 1.0, 1.0, ALU.mult, ALU.add, [B_const], [B_const])

        posi = carve([128, T], I32)
        ang = carve([128, T])
        kf = carve([128, T])
        ki = carve([128, T], I32)
        m1 = carve([128, T])
        tmp = carve([128, T])
        Bp, Ba, Bk, Bi, Bm, Bt2 = Buf(), Buf(), Buf(), Buf(), Buf(), Buf()
        load(posi, pos_in.ap().partition_broadcast(128), [], [Bp])
        cpy("dve", kf, posi, [Bp], [Bk])
        ts("dve", ang, kf, invf[:, 0:1], None, ALU.mult, None, [Bk, B_const], [Ba])

        def reduce_sin(dst, shift):
            ts("dve", kf, ang, shift, 1.0 / TWO_PI, ALU.add, ALU.mult, [Ba], [Bk])
            cpy("dve", ki, kf, [Bk], [Bi])
            cpy("dve", kf, ki, [Bi], [Bk])
            stt("dve", m1, kf, -TWO_PI, ang, ALU.mult, ALU.add, [Bk, Ba], [Bm])
            if shift != 0.0:
                ts("dve", m1, m1, shift, None, ALU.add, None, [Bm], [Bm])
            ts("dve", tmp, m1, math.pi, -TWO_PI, ALU.is_gt, ALU.mult, [Bm], [Bt2])
            tt("dve", m1, m1, tmp, ALU.add, [Bm, Bt2], [Bm])
            ts("dve", tmp, m1, -math.pi, TWO_PI, ALU.is_lt, ALU.mult, [Bm], [Bt2])
            tt("dve", m1, m1, tmp, ALU.add, [Bm, Bt2], [Bm])
            act(dst, m1, AF.Sin, [Bm], [B_const])

        reduce_sin(sinT[:], 0.0)
        reduce_sin(cosT[:], math.pi / 2)

        stg = [sb(f"stg{i}", [128, 2048], BF16) for i in range(3)]
        B_stg = [Buf() for _ in range(3)]
        cvs = {"i": 0}

        def weight_prep(l):
            for (n, R_, C_) in WS:
                rows = R_ // NCORES
                pp = 128 if rows % 128 == 0 else 86
                an = rows // pp
                nblk = 8 if C_ > 4096 else 2
                cb = C_ // nblk
                src = w_sh[n][l].rearrange("(a p) c -> a p c", p=pp)
                dst = wsh32[(l, n)].ap().rearrange("(a p) c -> a p c", p=pp)
                pieces = []
                jobs = [(a_, b_) for a_ in range(an) for b_ in range(nblk)]
                slots = []

                def do_load(a_, b_):
                    si = cvs["i"] % 3
                    cvs["i"] += 1
                    dma("pool", stg[si][0:pp, 0:cb], src[a_][:, b_ * cb:(b_ + 1) * cb], [], [B_stg[si]], cvl_ring)
                    return si

                def do_store(a_, b_, si):
                    bp = Buf()
                    pieces.append(bp)
                    dma("pool", dst[a_][:, b_ * cb // 2:(b_ + 1) * cb // 2], stg[si][0:pp, 0:cb].bitcast(F32), [B_stg[si]], [bp], cv_ring)

                for idx, (a_, b_) in enumerate(jobs):
                    slots.append(do_load(a_, b_))
                    if idx >= 2:
                        do_store(*jobs[idx - 2], slots[idx - 2])
                for idx in range(max(0, len(jobs) - 2), len(jobs)):
                    do_store(*jobs[idx], slots[idx])
                S.dma("pool", (lambda e_, a=wsh32[(l, n)], b=wfull32[(l, n)]: e_.collective_compute(
                    "AllGather", ALU.bypass, replica_groups=[list(range(NCORES))], ins=[a.ap().opt()], outs=[b.ap().opt()])),
                    pieces, [B_wfull[(l, n)]], cc_ring, inc=1)

        weight_prep(layers[0])

        wstate = {"i": 0}

        def wview(l, name):
            return wfull32[(l, name)].ap().rearrange("(c p) n -> p c n", p=128), B_wfull[(l, name)]

        def wload(parts, p, kp):
            si = wstate["i"] % NW
            wstate["i"] += 1
            bufs = []
            for (lo, ncol, (view, bsrc), c0) in parts:
                if ncol == 512:
                    bb = [B_w[si][0], B_w[si][1]]
                else:
                    bb = [B_w[si][0 if lo == 0 else 1]]
                dma("sp", wslotf[si][:, 0:kp, lo // 2:(lo + ncol) // 2], view[:, p * KP:p * KP + kp, c0 // 2:(c0 + ncol) // 2],
                    [bsrc], bb, w_ring)
                bufs += bb
            return wslot[si], bufs

        def gemm(mode, nk, parts, actfn, post, pre_piece=None):
            npieces = (nk + KP - 1) // KP
            for p in range(npieces):
                kp = min(KP, nk - p * KP)
                if pre_piece is not None:
                    pre_piece(p, kp)
                sl, wb = wload(parts, p, kp)
                for kk in range(kp):
                    k = p * KP + kk
                    a_ap, a_b = actfn(k)
                    first_k, last_k = (k == 0), (k == nk - 1)
                    for q in range(8):
                        if mode == "fm":
                            j, hf = q // 2, q % 2
                            lhsT = sl[:, kk, j * 128:(j + 1) * 128]
                            rhs = a_ap[:, hf * 512:(hf + 1) * 512]
                        else:
                            lhsT = a_ap[:, q * 128:(q + 1) * 128]
                            rhs = sl[:, kk, 0:512]
                        inc = last_k or (kk == kp - 1 and q == 7)
                        mm(banks[q][:, :], lhsT, rhs, first_k, last_k, wb + a_b, [B_bank[q]], inc=inc)
            post()

        def act_resident(k):
            return actT[:, k, :], [B_actT]

        def norm_to_actT(src, src_bufs, gain_ap):
            arena_reset()
            xts = [carve([128, D]) for _ in range(2)]
            xns = [carve([128, D], BF16) for _ in range(2)]
            Bx = [Buf(), Buf()]
            Bn = [Buf(), Buf()]
            ss = carve([128, 8])
            Bs = Buf()
            srcv = src.ap().rearrange("(i p) d -> i p d", p=128)
            for i in range(NT):
                s = i % 2
                load(xts[s], srcv[i], src_bufs, [Bx[s]])
                act(xns[s], xts[s], AF.Square, [Bx[s]], [Bn[s], Bs], accum=ss[:, i:i + 1])
                rstd_from(ss[:, i:i + 1], D, ss[:, i:i + 1], Bs, Bs)
                ts("dve", xns[s], xts[s], ss[:, i:i + 1], None, ALU.mult, None, [Bx[s], Bs], [Bn[s]])
                for g in range(4):
                    bk = (i * 4 + g) % 8
                    pv = banks[bk][:, :].bitcast(BF16)
                    for c in range(8):
                        kc = g * 8 + c
                        tr(pv[:, c * 128:(c + 1) * 128], xns[s][:, kc * 128:(kc + 1) * 128], [Bn[s]], [B_bank[bk]], inc=(c == 7))
                    eng = "dve" if g % 2 == 0 else "act"
                    o = actT[:, g * 8:(g + 1) * 8, i * 128:(i + 1) * 128]
                    pin = pv.rearrange("p (c t) -> p c t", t=128)
                    gb = gain_ap[:, g * 8:(g + 1) * 8].rearrange("p (c o) -> p c o", o=1).to_broadcast([128, 8, 128])
                    tt("dve", o, pin, gb, ALU.mult, [B_bank[bk], B_const], [B_actT])

        def make_resid_post(xsrc, nb, lastlayer_final=False):
            def post():
                xv = xsrc.ap().rearrange("(i p) d -> p i d", p=128)
                ov = xs_d.ap().rearrange("(i p) d -> p i d", p=128)
                for hf in range(2):
                    xr = resid_slots[resid_state["i"] % len(resid_slots)]
                    Br = resid_bufs[resid_state["i"] % len(resid_slots)]
                    resid_state["i"] += 1
                    rb = [B_xs[nb]] if xsrc is xs_d else []
                    load(xr, xv[:, hf * 4:(hf + 1) * 4, nb * 512:(nb + 1) * 512], rb, [Br])
                    for ii in range(4):
                        i = hf * 4 + ii
                        tt("dve", xr[:, ii, :], banks[i][:, :], xr[:, ii, :], ALU.add, [B_bank[i], Br], [Br])
                    store(ov[:, hf * 4:(hf + 1) * 4, nb * 512:(nb + 1) * 512], xr, [Br], [B_xs[nb]])
            return post

        resid_slots = []
        resid_bufs = []
        resid_state = {"i": 0}

        def carve_resid(n=3):
            resid_slots.clear()
            resid_bufs.clear()
            for _ in range(n):
                resid_slots.append(carve([128, 4, 512]))
                resid_bufs.append(Buf())

        def chk(tag):
            if stop_at == tag:
                raise _Stop()

        def run_layers():
            for l in layers:
                xsrc = x_in if l == layers[0] else xs_d
                xsrc_bufs = [] if xsrc is x_in else B_xs
                wv_in = wview(l, "w_in")

                norm_to_actT(xsrc, xsrc_bufs, mixn[:, l, :])

                chk("norm1")
                arena_reset()
                qT = carve([128, 2, T], BF16)
                kT = carve([128, 2, T], BF16)
                kdT = carve([128, 2, T], BF16)
                ktok = carve([128, NT, 256], BF16)
                vtok = carve([128, NT, 512], BF16)
                cpt = carve([128, 2, T])
                gaT = carve([128, T])
                lpt = carve([128, T])
                rt = [carve([128, 512]) for _ in range(4)]
                et = [carve([128, 512]) for _ in range(2)]
                Ust = carve([128, 2, 512])
                Sb = carve([128, 2, 512], BF16)
                Am = [carve([128, 128], BF16) for _ in range(2)]
                olr = [carve([128, 512]) for _ in range(3)]
                sgr = [carve([128, 512], BF16) for _ in range(3)]
                wup = carve([128, 1024])
                B_wup = Buf()
                load(wup[0:16, :], wup_d[l], [], [B_wup])
                lsum = carve([128, 8])
                send = carve([128, 2, 256])
                B_qT, B_kT, B_kdT, B_ktok, B_vtok, B_cp, B_ga, B_lp = (Buf() for _ in range(8))
                B_rt = [Buf() for _ in range(4)]
                B_et = [Buf() for _ in range(2)]
                B_U, B_Sb, B_ls, B_send = Buf(), Buf(), Buf(), Buf()
                B_Am = [Buf(), Buf()]
                B_olr = [Buf() for _ in range(3)]
                B_sgr = [Buf() for _ in range(3)]
                cnt = {"ol": 0, "sg": 0, "am": 0}
                agin = agin_d[l]

                def ga_post():
                    for hf in range(2):
                        cpy("dve", gaT[0:16, hf * 512:(hf + 1) * 512], banks[hf][0:16, :], [B_bank[hf]], [B_ga])

                npieces = KC // KP
                for p in range(npieces):
                    si = wstate["i"] % NW
                    wstate["i"] += 1
                    sl = wslot[si]
                    dma("sp", wslotf[si][:, 0:KP, 0:8], wv_in[0][:, p * KP:(p + 1) * KP, O_GA // 2:O_GA // 2 + 8], [wv_in[1]], [B_w[si][0]], w_ring)
                    for kk in range(KP):
                        k = p * KP + kk
                        for hf in range(2):
                            mm(banks[hf][0:16, :], sl[:, kk, 0:16], actT[:, k, hf * 512:(hf + 1) * 512], k == 0, k == KC - 1,
                               [B_w[si][0], B_actT], [B_bank[hf]], inc=(k == KC - 1 or (kk == KP - 1 and hf == 1)))
                ga_post()
                chk("ga")

                for j in range(GH):
                    hidx = RH + j
                    for e in range(2):
                        c = j * 2 + e
                        for hf in range(2):
                            bk = hf
                            mm(banks[bk][:, :], wup[0:16, c * 128:(c + 1) * 128], gaT[0:16, hf * 512:(hf + 1) * 512], True, True,
                               [B_wup, B_ga], [B_bank[bk]])
                            act(lpt[:, hf * 512:(hf + 1) * 512], banks[bk][:, :], AF.Exp, [B_bank[bk], B_const], [B_lp],
                                scale=-1.0, bias=negb[:, l, c:c + 1])
                        act(lpt, lpt, AF.Ln, [B_lp], [B_lp], scale=1.0, bias=1.0)
                        S.op("dve", (lambda e_, o=cpt[:, e, :]: e_.tensor_tensor_scan(out=o, data0=scanm[:], data1=lpt, initial=0.0,
                                                                                      op0=ALU.mult, op1=ALU.add)),
                             [B_lp, B_const], [B_cp])
                        lastv = cpt[:, e, :].rearrange("p (n r) -> p n r", r=128)[:, :, 127:128]
                        act(elall[:, c, :].rearrange("p (n o) -> p n o", o=1), lastv, AF.Exp, [B_cp], [B_elall], scale=-1.0 / 16)
                        S.op("dve", (lambda e_, o=lsum[:, c:c + 1], i_=lastv.rearrange("p n o -> p (n o)"): e_.reduce_sum(out=o, in_=i_, axis=mybir.AxisListType.X)),
                             [B_cp], [B_ls])
                    parts = [(0, 256, wv_in, O_GQ + j * 256), (256, 256, wv_in, O_GK + j * 256)]

                    def qk_post_gla():
                        for e in range(2):
                            for hf in range(2):
                                sl_ = slice(hf * 512, (hf + 1) * 512)
                                bq = e * 2 + hf
                                bkk = (2 + e) * 2 + hf
                                t0 = et[0]
                                act(t0, cpt[:, e, sl_], AF.Exp, [B_cp], [B_et[0]], scale=-1.0 / 16, bias=math.log(1.0 / 16))
                                tt("dve", qT[:, e, sl_], banks[bq][:, :], t0, ALU.mult, [B_bank[bq], B_et[0]], [B_qT])
                                t1 = et[1]
                                act(t1, cpt[:, e, sl_], AF.Exp, [B_cp], [B_et[1]], scale=1.0 / 16)
                                tt("dve", kT[:, e, sl_], banks[bkk][:, :], t1, ALU.mult, [B_bank[bkk], B_et[1]], [B_kT])
                                cv = cpt[:, e, sl_].rearrange("p (n r) -> p n r", r=128)
                                tt("dve", rt[0].rearrange("p (n r) -> p n r", r=128), cv, cv[:, :, 127:128].to_broadcast([128, 4, 128]),
                                   ALU.subtract, [B_cp], [B_rt[0]])
                                act(rt[0], rt[0], AF.Exp, [B_rt[0]], [B_rt[0]], scale=1.0 / 16)
                                tt("dve", kdT[:, e, sl_], banks[bkk][:, :], rt[0], ALU.mult, [B_bank[bkk], B_rt[0]], [B_kdT])
                        store(qt_d[hidx], qT, [B_qT], [B_qt[hidx]])

                    gemm("fm", KC, parts, act_resident, qk_post_gla)
                    for i in range(NT):
                        bk = i % 2
                        pv = banks[bk][:, :].bitcast(BF16)
                        for e in range(2):
                            tr(pv[:, e * 128:(e + 1) * 128], kdT[:, e, i * 128:(i + 1) * 128], [B_kdT], [B_bank[bk]], inc=(e == 1))
                        cpy("act", ktok[:, i, :], pv[:, 0:256], [B_bank[bk]], [B_ktok])
                    def v_post_gla():
                        for i in range(NT):
                            cpy("act" if i % 2 else "dve", vtok[:, i, :], banks[i][:, :], [B_bank[i]], [B_vtok])

                    gemm("tm", KC, [(0, 512, wv_in, O_GV + j * 512)], act_resident, v_post_gla)

                    def g_post_gla(col0=2048 + j * 512, hb=hidx):
                        sgv = sg_d.ap().rearrange("(i p) d -> i p d", p=128)
                        for i in range(NT):
                            s = cnt["sg"] % 3
                            cnt["sg"] += 1
                            act(sgr[s], banks[i][:, :], AF.Silu, [B_bank[i]], [B_sgr[s]])
                            store(sgv[i][:, col0:col0 + 512], sgr[s], [B_sgr[s]], [B_sg[hb]])

                    gemm("tm", KC, [(0, 512, wv_in, O_GG + j * 512)], act_resident, g_post_gla)

                    memset("dve", Ust, 0.0, [B_U])
                    olv = oloc_d.ap().rearrange("(i p) d -> i p d", p=128)
                    for n in range(NT):
                        tsl = slice(n * 128, (n + 1) * 128)
                        ab = n % 2
                        s_am = cnt["am"] % 2
                        cnt["am"] += 1
                        for e in range(2):
                            mm(banks[ab][:, 0:128], kT[:, e, tsl], qT[:, e, tsl], e == 0, e == 1, [B_kT, B_qT], [B_bank[ab]], inc=(e == 1))
                        tt("dve", Am[s_am], banks[ab][:, 0:128], cmask[:], ALU.mult, [B_bank[ab], B_const], [B_Am[s_am]])
                        ob = 2 + n % 2
                        mm(banks[ob][:, :], Am[s_am], vtok[:, n, :], True, n == 0, [B_Am[s_am], B_vtok], [B_bank[ob]], inc=(n == 0))
                        if n > 0:
                            for e in range(2):
                                mm(banks[ob][:, :], qT[:, e, tsl], Sb[:, e, :], False, e == 1, [B_qT, B_Sb], [B_bank[ob]], inc=(e == 1))
                        s_ol = cnt["ol"] % 3
                        cnt["ol"] += 1
                        cpy("act", olr[s_ol], banks[ob][:, :], [B_bank[ob]], [B_olr[s_ol]])
                        store(olv[n][:, 2048 + j * 512:2048 + (j + 1) * 512], olr[s_ol], [B_olr[s_ol]], [B_oloc[hidx]])
                        for e in range(2):
                            pb = 4 + (n % 2) * 2 + e
                            mm(banks[pb][:, :], ktok[:, n, e * 128:(e + 1) * 128], vtok[:, n, :], True, True, [B_ktok, B_vtok], [B_bank[pb]])
                            stt("dve", Ust[:, e, :], Ust[:, e, :], elall[:, j * 2 + e, n:n + 1], banks[pb][:, :], ALU.mult, ALU.add,
                                [B_U, B_elall, B_bank[pb]], [B_U])
                            if n < NT - 1:
                                cpy("act", Sb[:, e, :], Ust[:, e, :], [B_U], [B_Sb])
                    for e in range(2):
                        store(agin[:, 4096 + j * 1024 + e * 512:4096 + j * 1024 + (e + 1) * 512], Ust[:, e, :], [B_U], [B_agin[l]])

                chk("gla")
                act(lsum, lsum, AF.Exp, [B_ls], [B_ls], scale=-1.0 / 16)
                store(agin[:, 8192:8200], lsum, [B_ls], [B_agin[l]])

                for h in range(RH):
                    g128 = GAM[h] ** 128
                    parts = [(0, 256, wv_in, O_RQ + h * 256), (256, 256, wv_in, O_RK + h * 256)]

                    def qk_post_ret(h=h):
                        for which, dst, Bd, gt in ((0, qT, B_qT, gq_t), (1, kT, B_kT, gk_t)):
                            for hf in range(2):
                                sl_ = slice(hf * 512, (hf + 1) * 512)
                                b1 = (which * 2 + 0) * 2 + hf
                                b2 = (which * 2 + 1) * 2 + hf
                                gbc = gt[:, h:h + 1, :].to_broadcast([128, 4, 128])
                                tt("dve", rt[0], banks[b1][:, :], cosT[:, sl_], ALU.mult, [B_bank[b1], B_const], [B_rt[0]])
                                tt("dve", rt[1], banks[b2][:, :], sinT[:, sl_], ALU.mult, [B_bank[b2], B_const], [B_rt[1]])
                                tt("dve", rt[2], banks[b1][:, :], sinT[:, sl_], ALU.mult, [B_bank[b1], B_const], [B_rt[2]])
                                tt("dve", rt[3], banks[b2][:, :], cosT[:, sl_], ALU.mult, [B_bank[b2], B_const], [B_rt[3]])
                                tt("dve", rt[0], rt[0], rt[1], ALU.subtract, [B_rt[0], B_rt[1]], [B_rt[0]])
                                tt("dve", rt[2], rt[2], rt[3], ALU.add, [B_rt[2], B_rt[3]], [B_rt[2]])
                                tt("dve", dst[:, 0, sl_].rearrange("p (n r) -> p n r", r=128), rt[0].rearrange("p (n r) -> p n r", r=128),
                                   gbc, ALU.mult, [B_rt[0], B_const], [Bd])
                                tt("dve", dst[:, 1, sl_].rearrange("p (n r) -> p n r", r=128), rt[2].rearrange("p (n r) -> p n r", r=128),
                                   gbc, ALU.mult, [B_rt[2], B_const], [Bd])
                        store(qt_d[h], qT, [B_qT], [B_qt[h]])

                    gemm("fm", KC, parts, act_resident, qk_post_ret)
                    for i in range(NT):
                        bk = i % 2
                        pv = banks[bk][:, :].bitcast(BF16)
                        for e in range(2):
                            tr(pv[:, e * 128:(e + 1) * 128], kT[:, e, i * 128:(i + 1) * 128], [B_kT], [B_bank[bk]], inc=(e == 1))
                        cpy("act", ktok[:, i, :], pv[:, 0:256], [B_bank[bk]], [B_ktok])

                    parts = [(0, 256, wv_in, O_RV + h * 256), (256, 256, wv_in, O_RG + h * 256)]

                    def vg_post_ret(h=h):
                        sgv = sg_d.ap().rearrange("(i p) d -> i p d", p=128)
                        for i in range(NT):
                            cpy("act", vtok[:, i, 0:256], banks[i][:, 0:256], [B_bank[i]], [B_vtok])
                            s = cnt["sg"] % 3
                            cnt["sg"] += 1
                            act(sgr[s][:, 0:256], banks[i][:, 256:512], AF.Silu, [B_bank[i]], [B_sgr[s]])
                            store(sgv[i][:, h * 256:(h + 1) * 256], sgr[s][:, 0:256], [B_sgr[s]], [B_sg[h]])

                    gemm("tm", KC, parts, act_resident, vg_post_ret)

                    memset("dve", Ust, 0.0, [B_U])
                    olv = oloc_d.ap().rearrange("(i p) d -> i p d", p=128)
                    for n in range(NT):
                        tsl = slice(n * 128, (n + 1) * 128)
                        ab = n % 2
                        s_am = cnt["am"] % 2
                        cnt["am"] += 1
                        for e in range(2):
                            mm(banks[ab][:, 0:128], kT[:, e, tsl], qT[:, e, tsl], e == 0, e == 1, [B_kT, B_qT], [B_bank[ab]], inc=(e == 1))
                        tt("dve", Am[s_am], banks[ab][:, 0:128], cmask[:], ALU.mult, [B_bank[ab], B_const], [B_Am[s_am]])
                        ob = 2 + n % 2
                        mm(banks[ob][:, 0:256], Am[s_am], vtok[:, n, 0:256], True, n == 0, [B_Am[s_am], B_vtok], [B_bank[ob]], inc=(n == 0))
                        if n > 0:
                            for e in range(2):
                                mm(banks[ob][:, 0:256], qT[:, e, tsl], Sb[:, e, 0:256], False, e == 1, [B_qT, B_Sb], [B_bank[ob]], inc=(e == 1))
                        s_ol = cnt["ol"] % 3
                        cnt["ol"] += 1
                        cpy("act", olr[s_ol][:, 0:256], banks[ob][:, 0:256], [B_bank[ob]], [B_olr[s_ol]])
                        store(olv[n][:, h * 256:(h + 1) * 256], olr[s_ol][:, 0:256], [B_olr[s_ol]], [B_oloc[h]])
                        pb = 4 + n % 2
                        for e in range(2):
                            mm(banks[pb][:, e * 256:(e + 1) * 256], ktok[:, n, e * 128:(e + 1) * 128], vtok[:, n, 0:256], True, True,
                               [B_ktok, B_vtok], [B_bank[pb]], inc=(e == 1))
                        uv = Ust[:, :, 0:256]
                        pvw = banks[pb][:, :].rearrange("p (e v) -> p e v", v=256)
                        stt("dve", uv, uv, g128, pvw, ALU.mult, ALU.add, [B_U, B_bank[pb]], [B_U])
                        if n < NT - 1:
                            ts("dve", Sb[:, :, 0:256], uv, g128, None, ALU.mult, None, [B_U], [B_Sb])
                    ts("dve", send, Ust[:, :, 0:256], g128, None, ALU.mult, None, [B_U], [B_send])
                    store(agin[:, h * 512:(h + 1) * 512].rearrange("p (e v) -> p e v", v=256), send, [B_send], [B_agin[l]])

                chk("scan")
                S.dma("pool", (lambda e_, a=agin_d[l], b=agout_d[l]: e_.collective_compute(
                    "AllGather", ALU.bypass, replica_groups=[list(range(NCORES))], ins=[a.ap().opt()], outs=[b.ap().opt()])),
                    [B_agin[l]], [B_agout[l]], cc_ring, inc=1)
                if l != layers[-1]:
                    weight_prep(layers[layers.index(l) + 1])

                chk("ag")
                arena_reset()
                agv = agout_d[l].ap().rearrange("(r p) f -> p r f", p=128)
                Sall = carve([128, 8, 512])
                acc = carve([128, 512])
                tmpS = carve([128, 512])
                Sinb = carve([128, 512], BF16)
                olt = carve([128, NT, 512])
                sgl = carve([128, NT, 512], BF16)
                qTl = carve([128, 2, T], BF16)
                oft = [carve([128, 512]) for _ in range(2)]
                yt = [carve([128, 512]) for _ in range(2)]
                mxt = [carve([128, 512], BF16) for _ in range(2)]
                rgain = carve([128, 2048])
                ggain = carve([128, 2048])
                Aall = carve([128, 8, 8])
                ajg = carve([128, 8, 8])
                stat = carve([128, 8])
                bst = carve([128, 8])
                B_Sall, B_acc, B_tmpS, B_Sinb, B_olt, B_sgl, B_qTl = (Buf() for _ in range(7))
                B_of = [Buf(), Buf()]
                B_y = [Buf(), Buf()]
                B_mx = [Buf(), Buf()]
                B_gain, B_A, B_ajg, B_stat, B_bst = (Buf() for _ in range(5))
                load(rgain, rgain_d[l].partition_broadcast(128), [], [B_gain])
                load(ggain, ggain_d[l].partition_broadcast(128), [], [B_gain])
                load(Aall, agv[:, :, 8192:8200], [B_agout[l]], [B_A])
                Ac = Aall.rearrange("p r c -> p c r")
                ts("dve", ajg, Ac, -1.0, None, ALU.add, None, [B_A], [B_ajg])
                tt("dve", ajg, ajg, corem[:].rearrange("p (o r) -> p o r", o=1).to_broadcast([128, 8, 8]), ALU.mult, [B_ajg, B_const], [B_ajg])
                ts("dve", ajg, ajg, 1.0, None, ALU.add, None, [B_ajg], [B_ajg])
                tcount = {"i": 0}

                for hidx in list(range(RH, RH + GH)) + list(range(RH)):
                    is_ret = hidx < RH
                    dv = 256 if is_ret else 512
                    col0 = hidx * 256 if is_ret else 2048 + (hidx - RH) * 512
                    load(olt[:, :, 0:dv], oloc_d.ap().rearrange("(i p) d -> p i d", p=128)[:, :, col0:col0 + dv], [B_oloc[hidx]], [B_olt])
                    load(sgl[:, :, 0:dv], sg_d.ap().rearrange("(i p) d -> p i d", p=128)[:, :, col0:col0 + dv], [B_sg[hidx]], [B_sgl])
                    load(qTl, qt_d[hidx], [B_qt[hidx]], [B_qTl])
                    for e in range(2):
                        if is_ret:
                            src = agv[:, :, hidx * 512 + e * 256:hidx * 512 + (e + 1) * 256]
                        else:
                            j = hidx - RH
                            src = agv[:, :, 4096 + j * 1024 + e * 512:4096 + j * 1024 + (e + 1) * 512]
                        load(Sall[:, :, 0:dv], src, [B_agout[l]], [B_Sall])
                        memset("dve", acc[:, 0:dv], 0.0, [B_acc])
                        for r in range(NCORES - 1):
                            ts("dve", tmpS[:, 0:dv], Sall[:, r, 0:dv], corem[:, r:r + 1], None, ALU.mult, None, [B_Sall, B_const], [B_tmpS])
                            sc = ajr[:, hidx, r:r + 1] if is_ret else ajg[:, (hidx - RH) * 2 + e, r:r + 1]
                            stt("dve", acc[:, 0:dv], acc[:, 0:dv], sc, tmpS[:, 0:dv], ALU.mult, ALU.add,
                                [B_acc, B_tmpS, B_const, B_ajg], [B_acc])
                        for n in range(NT):
                            if is_ret:
                                ts("dve", Sinb[:, 0:dv], acc[:, 0:dv], GAM[hidx] ** (128 * n), None, ALU.mult, None, [B_acc], [B_Sinb])
                            else:
                                cpy("dve", Sinb[:, 0:dv], acc[:, 0:dv], [B_acc], [B_Sinb])
                            mm(banks[n][:, 0:dv], qTl[:, e, n * 128:(n + 1) * 128], Sinb[:, 0:dv], e == 0, e == 1, [B_qTl, B_Sinb], [B_bank[n]])
                            if (not is_ret) and n < NT - 1:
                                ts("dve", acc[:, 0:dv], acc[:, 0:dv], elall[:, (hidx - RH) * 2 + e, n:n + 1], None, ALU.mult, None,
                                   [B_acc, B_elall], [B_acc])
                    gain = rgain if is_ret else ggain
                    gcol = hidx * 256 if is_ret else (hidx - RH) * 512
                    for n in range(NT):
                        s = tcount["i"] % 2
                        tcount["i"] += 1
                        of, y, mx = oft[s][:, 0:dv], yt[s][:, 0:dv], mxt[s][:, 0:dv]
                        tt("dve", of, banks[n][:, 0:dv], olt[:, n, 0:dv], ALU.add, [B_bank[n], B_olt], [B_of[s]])
                        if is_ret:
                            S.op("dve", (lambda e_, o=bst[:, 0:6], i_=of: e_.bn_stats(out=o, in_=i_)), [B_of[s]], [B_bst])
                            S.op("dve", (lambda e_, o=stat[:, 0:2], i_=bst[:, 0:6]: e_.bn_aggr(out=o, in_=i_)), [B_bst], [B_stat])
                            act(stat[:, 2:3], stat[:, 1:2], AF.Ln, [B_stat, B_const], [B_stat], scale=1.0, bias=epsb[:, 0:1])
                            act(stat[:, 2:3], stat[:, 2:3], AF.Exp, [B_stat], [B_stat], scale=-0.5)
                            ts("dve", y, of, stat[:, 0:1], stat[:, 2:3], ALU.subtract, ALU.mult, [B_of[s], B_stat], [B_y[s]])
                        else:
                            act(y, of, AF.Square, [B_of[s]], [B_y[s], B_stat], accum=stat[:, 4:5])
                            rstd_from(stat[:, 4:5], dv, stat[:, 5:6], B_stat, B_stat)
                            ts("dve", y, of, stat[:, 5:6], None, ALU.mult, None, [B_of[s], B_stat], [B_y[s]])
                        tt("dve", y, y, gain[:, gcol:gcol + dv], ALU.mult, [B_y[s], B_gain], [B_y[s]])
                        tt("dve", mx, y, sgl[:, n, 0:dv], ALU.mult, [B_y[s], B_sgl], [B_mx[s]])
                        pv = banks[n][:, :].bitcast(BF16)
                        nch = dv // 128
                        for c in range(nch):
                            tr(pv[:, c * 128:(c + 1) * 128], mx[:, c * 128:(c + 1) * 128], [B_mx[s]], [B_bank[n]], inc=(c == nch - 1))
                        kc0 = col0 // 128
                        cpy("act", actT[:, kc0:kc0 + nch, n * 128:(n + 1) * 128],
                            pv[:, 0:dv].rearrange("p (c t) -> p c t", t=128), [B_bank[n]], [B_actT])

                chk("corr")
                arena_reset()
                carve_resid(3)
                wv_out = wview(l, "w_out")
                for nb in range(8):
                    gemm("tm", KC, [(0, 512, wv_out, nb * 512)], act_resident, make_resid_post(xsrc, nb))

                if skip_ffn:
                    continue
                norm_to_actT(xs_d, B_xs, ffnn[:, l, :])

                arena_reset()
                sil = [carve([128, 512]) for _ in range(4)]
                B_sil = [Buf() for _ in range(4)]
                ablk = [carve([128, 2, T], BF16) for _ in range(2)]
                B_ablk = [Buf(), Buf()]
                wv_g = wview(l, "w_gate")
                wv_u = wview(l, "w_up")
                actv = act_d.ap().rearrange("f p t -> p f t")
                fcnt = {"i": 0}
                for fb in range(FCH // 2):
                    parts = [(0, 256, wv_g, fb * 256), (256, 256, wv_u, fb * 256)]

                    def ffn_a_post(fb=fb):
                        ab = ablk[fb % 2]
                        Bab = B_ablk[fb % 2]
                        for jj in range(2):
                            for hf in range(2):
                                s = fcnt["i"] % 4
                                fcnt["i"] += 1
                                bg = jj * 2 + hf
                                bu = (2 + jj) * 2 + hf
                                act(sil[s], banks[bg][:, :], AF.Silu, [B_bank[bg]], [B_sil[s]])
                                tt("dve", ab[:, jj, hf * 512:(hf + 1) * 512], banks[bu][:, :], sil[s], ALU.mult, [B_bank[bu], B_sil[s]], [Bab])
                        store(actv[:, fb * 2:(fb + 1) * 2, :], ab, [Bab], [B_act[fb]])

                    gemm("fm", KC, parts, act_resident, ffn_a_post)

                arena_reset()
                carve_resid(3)
                NA = 4
                aslot = [carve([128, KP, T], BF16) for _ in range(NA)]
                B_as = [Buf() for _ in range(NA)]
                wv_d = wview(l, "w_down")
                astate = {"i": 0, "cur": None}
                for nb in range(8):
                    def pre_piece(p, kp):
                        si = astate["i"] % NA
                        astate["i"] += 1
                        load(aslot[si][:, 0:kp, :], actv[:, p * KP:p * KP + kp, :], [B_act[(p * KP) // 2], B_act[(p * KP + kp - 1) // 2]], [B_as[si]])
                        astate["cur"] = (si, p)

                    def act_piece(k):
                        si, p = astate["cur"]
                        return aslot[si][:, k - p * KP, :], [B_as[si]]

                    gemm("tm", FCH, [(0, 512, wv_d, nb * 512)], act_piece, make_resid_post(xs_d, nb), pre_piece=pre_piece)


        try:
            chk("init")
            run_layers()
        except _Stop:
            pass

        if last:
            arena_reset()
            xts = [carve([128, D]) for _ in range(2)]
            jk = carve([128, D], BF16)
            fin = carve([128, D])
            Bx = [Buf(), Buf()]
            Bj, Bf, Bs = Buf(), Buf(), Buf()
            ss = carve([128, 8])
            load(fin, finn_d.ap().partition_broadcast(128), [], [Bf])
            srcv = xs_d.ap().rearrange("(i p) d -> i p d", p=128)
            outv = out_d.ap().rearrange("(i p) d -> i p d", p=128)
            for i in range(NT):
                s = i % 2
                load(xts[s], srcv[i], B_xs, [Bx[s]])
                act(jk, xts[s], AF.Square, [Bx[s]], [Bj, Bs], accum=ss[:, i:i + 1])
                rstd_from(ss[:, i:i + 1], D, ss[:, i:i + 1], Bs, Bs)
                stt("dve", xts[s], xts[s], ss[:, i:i + 1], fin, ALU.mult, ALU.mult, [Bx[s], Bs, Bf], [Bx[s]])
                store(outv[i], xts[s], [Bx[s]], [B_out])
        else:
            arena_reset()
            xts = [carve([128, D]) for _ in range(2)]
            Bx = [Buf(), Buf()]
            srcv = xs_d.ap().rearrange("(i p) d -> i p d", p=128)
            outv = out_d.ap().rearrange("(i p) d -> i p d", p=128)
            for i in range(NT):
                s = i % 2
                load(xts[s], srcv[i], B_xs, [Bx[s]])
                store(outv[i], xts[s], [Bx[s]], [B_out])

        S.barrier(engines=("pe", "dve", "act", "sp", "pool"))
        S.emit(nc)
    build_program.stats = {n: len(E.prog) for n, E in S.E.items()}
    build_program.nsem = S.nsem
    return nc


def _consts():
    invf = (10000.0 ** (-np.arange(128, dtype=np.float32) / 128.0)).astype(np.float32)[:, None]
    ident = np.eye(128, dtype=np.float32)
    jj = np.arange(128)
    cmask = (jj[None, :] >= jj[:, None]).astype(np.float32)
    scanm = np.ones((128, T), np.float32)
    scanm[:, ::128] = 0.0
    r = np.arange(128, dtype=np.float64)
    gq = np.zeros((RH, 128), np.float64)
    gk = np.zeros((RH, 128), np.float64)
    for h in range(RH):
        lg = math.log1p(-2.0 ** (-5.0 - h))
        gq[h] = np.exp(lg * (r + 1.0))
        gk[h] = np.exp(-lg * (r + 1.0)) / 16.0
    gq = np.broadcast_to(gq.reshape(1, RH * 128), (128, RH * 128)).astype(np.float32).copy()
    gk = np.broadcast_to(gk.reshape(1, RH * 128), (128, RH * 128)).astype(np.float32).copy()
    return invf, ident, cmask, scanm, gq, gk


_PROG = {}


def kernel(x, positions, mix_norm, w_in, gla_w_up, gla_b, ret_gain, gla_gain, w_out, ffn_norm,
           w_gate, w_up, w_down, final_norm):
    f32 = np.float32
    x = np.asarray(x, f32).reshape(SEQ, D)
    pos = np.asarray(positions, np.int32).reshape(SEQ)
    invf, ident, cmask, scanm, gq, gk = _consts()
    shared = {
        "mixn": np.ascontiguousarray(np.asarray(mix_norm, f32).reshape(DEPTH, KC, 128).transpose(0, 2, 1)),
        "ffnn": np.ascontiguousarray(np.asarray(ffn_norm, f32).reshape(DEPTH, KC, 128).transpose(0, 2, 1)),
        "finn": np.asarray(final_norm, f32).reshape(1, D),
        "rgain": np.asarray(ret_gain, f32).reshape(DEPTH, 1, 2048),
        "ggain": np.asarray(gla_gain, f32).reshape(DEPTH, 1, 2048),
        "glab": np.ascontiguousarray(np.asarray(gla_b, f32).reshape(DEPTH, 8, 128).transpose(0, 2, 1)),
        "wup": np.ascontiguousarray(gla_w_up, f32),
        "invf": invf, "ident": ident, "cmask": cmask, "scanm": scanm, "gq": gq, "gk": gk,
    }
    if "nc" not in _PROG:
        _PROG["nc"] = build_program(**_PROG.get("args", {}))
    nc = _PROG["nc"]
    in_maps = []
    for c in range(NCORES):
        m = dict(shared)
        m["x"] = np.ascontiguousarray(x[c * T:(c + 1) * T])
        m["pos"] = np.ascontiguousarray(pos[c * T:(c + 1) * T]).reshape(1, T)
        cm = np.zeros((128, 8), f32)
        cm[:, :c] = 1.0
        m["corem"] = cm
        for name, w in (("w_in", w_in), ("w_out", w_out), ("w_gate", w_gate), ("w_up", w_up), ("w_down", w_down)):
            w = np.asarray(w, f32)
            rs = w.shape[1] // NCORES
            m[name] = np.ascontiguousarray(w[:, c * rs:(c + 1) * rs, :])
        in_maps.append(m)
    res = run_bass_kernel_spmd(nc, in_maps, core_ids=list(range(NCORES)))
    out = np.concatenate([np.asarray(r["out"], f32) for r in res.results], axis=0)
    _PROG["res"] = res
    return out.reshape(1, SEQ, D)
```
